# Optimizing a Trainium2 kernel written in Bass

```python
import jax, jax.numpy as jnp
from jax import lax
import numpy as np

D_MODEL = 1024
BATCH = 8
SEQ = 4096
DEPTH = 2

CHUNK = 64
D_MIX = D_MODEL
A_WIDTH = D_MIX // 2
A_GROUPS = 8
A_GROUP_DIM = A_WIDTH // A_GROUPS
A_BLOCK = 128
B_WIDTH = D_MIX - A_WIDTH
B_HEADS = 4
B_HEAD_V = B_WIDTH // B_HEADS
B_HEAD_K = B_HEAD_V // 2
B_KEY = B_HEADS * B_HEAD_K
GATE_RANK = 16
GATE_NORM = 16.0
D_FF = 2816
EPS = 1e-6
SPLITS = (A_WIDTH, 2 * A_WIDTH, 2 * A_WIDTH + B_KEY, 2 * A_WIDTH + 2 * B_KEY,
          2 * A_WIDTH + 2 * B_KEY + B_WIDTH, 2 * A_WIDTH + 2 * B_KEY + 2 * B_WIDTH)
IN_COLS = 2 * A_WIDTH + 2 * B_KEY + 2 * B_WIDTH + GATE_RANK

kernel_name = "hybrid_gmlp_gla_macaron_encoder"


def rmsnorm(x, w):
    xf = x.astype(jnp.float32)
    y = xf * lax.rsqrt(jnp.mean(xf * xf, axis=-1, keepdims=True) + EPS)
    return (y * w.astype(jnp.float32)).astype(x.dtype)


def swiglu_ffn(h, w_in, w_out):
    a, gate = jnp.split(h @ w_in, 2, axis=-1)
    return (jax.nn.silu(gate) * a) @ w_out


def gmlp_spatial_gating(u, v, w_s, b_s, norm_v):
    bsz, seq, _ = u.shape
    nb = seq // A_BLOCK
    v = rmsnorm(v, norm_v).reshape(bsz, nb, A_BLOCK, A_GROUPS, A_GROUP_DIM)
    chunk_id = jnp.arange(A_BLOCK) // CHUNK
    mask = chunk_id[:, None] >= chunk_id[None, :]
    ws = jnp.where(mask[None], w_s, jnp.zeros_like(w_s))
    z = jnp.einsum('gts,bnsgc->bntgc', ws, v) + b_s.T[None, None, :, :, None].astype(v.dtype)
    return u * z.reshape(bsz, seq, A_WIDTH)


def gated_linear_attention(q, k, v, g, r, w_gk2, b_gk, norm_o):
    out_dtype = v.dtype
    bsz, seq, _ = q.shape
    nc = seq // CHUNK
    f32 = jnp.float32
    qc = q.astype(f32).reshape(bsz, nc, CHUNK, B_HEADS, B_HEAD_K) * (B_HEAD_K ** -0.5)
    kc = k.astype(f32).reshape(bsz, nc, CHUNK, B_HEADS, B_HEAD_K)
    vc = v.astype(f32).reshape(bsz, nc, CHUNK, B_HEADS, B_HEAD_V)
    log_a = jax.nn.log_sigmoid((r @ w_gk2 + b_gk).astype(f32)) / GATE_NORM
    log_a = log_a.reshape(bsz, nc, CHUNK, B_HEADS, B_HEAD_K)
    cum = jnp.cumsum(log_a, axis=2)
    tot = cum[:, :, -1]
    k_dec = kc * jnp.exp(tot[:, :, None] - cum)
    upd = jnp.einsum('bnthk,bnthv->bnhkv', k_dec, vc)
    decay = jnp.exp(tot)

    def step(state, inp):
        d, u = inp
        state = d[..., None] * state + u
        return state, state

    init = jnp.zeros((bsz, B_HEADS, B_HEAD_K, B_HEAD_V), f32)
    _, states = lax.scan(step, init, (jnp.moveaxis(decay, 1, 0), jnp.moveaxis(upd, 1, 0)))
    states = jnp.moveaxis(states, 0, 1)
    o = jnp.einsum('bnthk,bnhkv->bnthv', qc, states)
    o = o * lax.rsqrt(jnp.mean(o * o, axis=-1, keepdims=True) + EPS) * norm_o.astype(f32)
    o = o.reshape(bsz, seq, B_WIDTH).astype(out_dtype)
    return o * jax.nn.silu(g)


def setup_inputs(seed: int = 0) -> dict:
    key = jax.random.key(seed)
    ks = jax.random.split(key, 20)
    f32 = jnp.float32
    nrm = lambda k, shape, scale: jax.random.normal(k, shape, f32) * scale
    gain = lambda k, shape: 1.0 + 0.05 * jax.random.normal(k, shape, f32)
    L = DEPTH
    return {
        "x": nrm(ks[0], (BATCH, SEQ, D_MODEL), 1.0),
        "ff1_norm": gain(ks[1], (L, D_MODEL)),
        "ff1_w_in": nrm(ks[2], (L, D_MODEL, 2 * D_FF), D_MODEL ** -0.5),
        "ff1_w_out": nrm(ks[3], (L, D_FF, D_MODEL), D_FF ** -0.5),
        "mix_norm": gain(ks[4], (L, D_MODEL)),
        "w_in": nrm(ks[5], (L, D_MODEL, IN_COLS), D_MODEL ** -0.5),
        "gmlp_norm_v": gain(ks[6], (L, A_WIDTH)),
        "gmlp_w_s": nrm(ks[7], (L, A_GROUPS, A_BLOCK, A_BLOCK), A_BLOCK ** -0.5),
        "gmlp_b_s": 1.0 + 0.1 * jax.random.normal(ks[8], (L, A_GROUPS, A_BLOCK), f32),
        "gla_w_gk2": nrm(ks[9], (L, GATE_RANK, B_KEY), GATE_RANK ** -0.5),
        "gla_b_gk": nrm(ks[10], (L, B_KEY), 0.1),
        "gla_norm_o": gain(ks[11], (L, B_HEADS, B_HEAD_V)),
        "w_out": nrm(ks[12], (L, D_MIX, D_MODEL), D_MIX ** -0.5),
        "ff2_norm": gain(ks[13], (L, D_MODEL)),
        "ff2_w_in": nrm(ks[14], (L, D_MODEL, 2 * D_FF), D_MODEL ** -0.5),
        "ff2_w_out": nrm(ks[15], (L, D_FF, D_MODEL), D_FF ** -0.5),
        "final_norm": gain(ks[16], (D_MODEL,)),
    }


def reference(x, ff1_norm, ff1_w_in, ff1_w_out, mix_norm, w_in, gmlp_norm_v, gmlp_w_s,
              gmlp_b_s, gla_w_gk2, gla_b_gk, gla_norm_o, w_out, ff2_norm, ff2_w_in,
              ff2_w_out, final_norm):
    for l in range(DEPTH):
        x = x + 0.5 * swiglu_ffn(rmsnorm(x, ff1_norm[l]), ff1_w_in[l], ff1_w_out[l])
        h = rmsnorm(x, mix_norm[l])
        proj = h @ w_in[l]
        u, va, q, k, vb, g, r = jnp.split(proj, SPLITS, axis=-1)
        y_a = gmlp_spatial_gating(u, va, gmlp_w_s[l], gmlp_b_s[l], gmlp_norm_v[l])
        y_b = gated_linear_attention(q, k, vb, g, r, gla_w_gk2[l], gla_b_gk[l], gla_norm_o[l])
        x = x + jnp.concatenate([y_a, y_b], axis=-1) @ w_out[l]
        x = x + 0.5 * swiglu_ffn(rmsnorm(x, ff2_norm[l]), ff2_w_in[l], ff2_w_out[l])
    return rmsnorm(x, final_norm)
```

```python
import numpy as np
from contextlib import ExitStack
import concourse.bass as bass
import concourse.mybir as mybir
from concourse.bass_utils import run_bass_kernel_spmd

F32 = mybir.dt.float32
BF16 = mybir.dt.bfloat16
AF = mybir.ActivationFunctionType
ALU = mybir.AluOpType

L = 2
D = 1024
KC = 8
SEQ = 4096
T = 1024
NT = SEQ // T
HT = 512
FF = 2816
FC = 22
IN_COLS = 2576
EPS = 1e-6
NSLOT = 7
SLOT = 4096
N_CORES = 8


class Sched:
    ENG = ("pe", "act", "dve", "pool", "sp")

    def __init__(self, nc, es):
        self.nc = nc
        self.es = es
        self.q = {e: [] for e in self.ENG}
        self.sems = {}
        self.cnt = {}
        self.waited = {e: {} for e in self.ENG}
        self.lastw = {}
        self.readers = {}
        for e in self.ENG:
            self._sem(e)

    def _sem(self, name):
        if name not in self.sems:
            self.sems[name] = self.es.enter_context(self.nc.semaphore("s_" + name))
            self.cnt[name] = 0
        return self.sems[name]

    def _deps(self, eng, reads, writes):
        deps = {}

        def add(tok):
            if tok is None:
                return
            s, v = tok
            if s == eng and eng == "pe":
                return
            if deps.get(s, 0) < v:
                deps[s] = v
        for r in reads:
            add(self.lastw.get(r))
        for w in writes:
            add(self.lastw.get(w))
            for t in self.readers.get(w, ()):
                add(t)
        for s, v in deps.items():
            if self.waited[eng].get(s, 0) < v:
                self.waited[eng][s] = v
                sem = self.sems[s]
                self.q[eng].append(lambda h, sem=sem, v=v: h.wait_ge(sem, v))

    def _post(self, tok, reads, writes):
        for r in reads:
            lst = self.readers.setdefault(r, [])
            if not lst or lst[-1] != tok:
                lst.append(tok)
        for w in writes:
            self.lastw[w] = tok
            self.readers[w] = []

    def op(self, eng, fn, reads=(), writes=(), inc=True):
        self._deps(eng, reads, writes)
        sem = self.sems[eng]
        if inc:
            self.cnt[eng] += 1
            tok = (eng, self.cnt[eng])
            self.q[eng].append(lambda h, fn=fn, sem=sem: fn(h).then_inc(sem, 1))
        else:
            tok = (eng, self.cnt[eng] + 1)
            self.q[eng].append(lambda h, fn=fn: fn(h))
        self._post(tok, reads, writes)
        return tok

    def dma(self, eng, semname, fn, reads=(), writes=()):
        self._deps(eng, reads, writes)
        sem = self._sem(semname)
        self.cnt[semname] += 16
        tok = (semname, self.cnt[semname])
        self.q[eng].append(lambda h, fn=fn, sem=sem: fn(h).then_inc(sem, 16))
        self._post(tok, reads, writes)
        return tok

    def wait_all(self, eng, toks):
        for s, v in toks:
            if self.waited[eng].get(s, 0) < v:
                self.waited[eng][s] = v
                sem = self.sems[s]
                self.q[eng].append(lambda h, sem=sem, v=v: h.wait_ge(sem, v))

    def run(self):
        nc = self.nc
        with nc.Block() as block:
            @block.tensor
            def _(h):
                for f in self.q["pe"]:
                    f(h)

            @block.scalar
            def _(h):
                for f in self.q["act"]:
                    f(h)

            @block.vector
            def _(h):
                for f in self.q["dve"]:
                    f(h)

            @block.gpsimd
            def _(h):
                for f in self.q["pool"]:
                    f(h)

            @block.sync
            def _(h):
                for f in self.q["sp"]:
                    f(h)


def build(n_tiles=NT, n_layers=L):
    nc = bass.Bass("TRN2", target_bir_lowering=False)

    def din(name, shape):
        return nc.dram_tensor(name, shape, F32, kind="ExternalInput").ap()
    x = din("x", [SEQ, D])
    out = nc.dram_tensor("out", [SEQ, D], F32, kind="ExternalOutput").ap()
    ff1_norm = din("ff1_norm", [L, D])
    ff1_w_in = din("ff1_w_in", [L, D, 2 * FF])
    ff1_w_out = din("ff1_w_out", [L, FF, D])
    mix_norm = din("mix_norm", [L, D])
    w_in = din("w_in", [L, D, IN_COLS])
    gmlp_norm_v = din("gmlp_norm_v", [L, 512])
    gmlp_w_s = din("gmlp_w_s", [L, 8, 128, 128])
    gmlp_b_s = din("gmlp_b_s", [L, 8, 128])
    gla_w_gk2 = din("gla_w_gk2", [L, 16, 256])
    gla_b_gk = din("gla_b_gk", [L, 256])
    gla_norm_o = din("gla_norm_o", [L, 4, 128])
    w_out = din("w_out", [L, D, D])
    ff2_norm = din("ff2_norm", [L, D])
    ff2_w_in = din("ff2_w_in", [L, D, 2 * FF])
    ff2_w_out = din("ff2_w_out", [L, FF, D])
    final_norm = din("final_norm", [D])

    es = ExitStack()
    with es:
        S = Sched(nc, es)

        def sb(name, shape, dt):
            return es.enter_context(nc.sbuf_tensor(name, shape, dt))
        xT = sb("xT", [128, KC, T], F32)
        xn = sb("xn", [128, KC, T], BF16)
        hbuf = sb("hbuf", [128, 11264], F32)
        sqy = sb("sqy", [128, KC, HT], BF16)
        rstd = [sb(f"rstd{i}", [128, HT], F32) for i in range(2)]
        spa = sb("spa", [128, 4, 256], F32)
        sg = [sb(f"sg{i}", [128, HT], F32) for i in range(2)]
        at = [sb(f"at{i}", [128, HT], F32) for i in range(2)]
        ornd = sg
        lnb2 = at
        lnb = at[1]
        ring = [sb(f"ring{i}", [128, SLOT], BF16) for i in range(NSLOT)]
        stg = [sb(f"stg{i}", [128, D], F32) for i in range(2)]
        ident = sb("ident", [128, 128], F32)
        ones_bf = sb("ones_bf", [128, 128], BF16)
        tri = sb("tri", [128, 128], F32)
        ci = sb("ci", [128, 2], F32)
        dummy = sb("fdummy", [128, 8], F32)
        NNW = 3 * L + 1
        nw = sb("nw", [128, NNW * KC], F32)
        normo = sb("normo", [128, L * 4], F32)
        wsT = [sb(f"wsT{l}", [128, 8, 128], BF16) for l in range(L)]
        bsT = [sb(f"bsT{l}", [128, 4, 128], F32) for l in range(L)]
        normv = [sb(f"normv{l}", [128, 512], F32) for l in range(L)]
        bgk = [sb(f"bgk{l}", [128, 256], F32) for l in range(L)]
        wgk = [sb(f"wgk{l}", [16, 256], BF16) for l in range(L)]
        Sst = [sb(f"Sst{l}", [128, 2, 128], F32) for l in range(L)]
        Sbf = [sb(f"Sbf{i}", [128, 2, 128], BF16) for i in range(2)]
        dec = sb("dec", [128, 16], F32)
        vst = sb("vst", [128, 8], F32)
        PS = [es.enter_context(nc.psum_tensor(f"ps{i}", [128, 512], F32)) for i in range(8)]
        PSN = [f"ps{i}" for i in range(8)]

        def hv(off, words, dt):
            v = hbuf[:, off:off + words]
            return v.bitcast(dt) if dt is not F32 else v
        h_all = hv(0, 11264, BF16).rearrange("p (f t) -> p f t", f=FC)
        o = 0
        uT = hv(o, 2048, F32).rearrange("p (c t) -> p c t", c=4); o += 2048
        sgT = hv(o, 2048, F32).rearrange("p (c t) -> p c t", c=4); o += 2048
        vn = hv(o, 1024, BF16).rearrange("p (b n) -> p b n", b=4); o += 1024
        qT = hv(o, 512, BF16).rearrange("p (c t) -> p c t", c=2); o += 512
        wdec = hv(o, 1024, F32).rearrange("p (b n) -> p b n", b=4); o += 1024
        kdec = hv(o, 512, BF16).rearrange("p (b n) -> p b n", b=4); o += 512
        vtok = hv(o, 1024, BF16).rearrange("p (b n) -> p b n", b=4); o += 1024
        tmp = []
        for i in range(2):
            tmp.append(hv(o, 512, F32)); o += 512
        osq = []
        for i in range(2):
            osq.append(hv(o, 256, BF16)); o += 256
        spb = []
        for i in range(2):
            spb.append(hv(o, 256, F32)); o += 256
        xgb = []
        for i in range(2):
            xgb.append(hv(o, 256, F32)); o += 256
        rT = hv(o, 256, BF16); o += 256
        junk = hv(o, 256, BF16); o += 256
        assert o <= 11264 - 2048 + 2048
        yf = hv(0, 4096, F32).rearrange("p (c t) -> p c t", c=KC)
        yT = sb("yT", [128, KC, HT], BF16)
        HGRP = ["h0", "h1", "uT", "sgT", "vn", "qT", "wdec", "kdec0", "kdec1", "kdec2", "kdec3", "vtok0", "vtok1", "vtok2", "vtok3", "tmp0", "tmp1", "osq0", "osq1",
                "sp0", "sp1", "xg0", "xg1", "rT", "junk", "yf"]

        def fence():
            S.op("dve", lambda h: h.memset(dummy[:], 0.0), writes=HGRP)

        def MM(out, lhsT, rhs, start, stop, reads, writes, inc, **kw):
            S.op("pe", lambda h: h.matmul(out, lhsT=lhsT, rhs=rhs, start=start, stop=stop, **kw), reads, writes, inc)

        def TR(out, in_, reads, writes, inc):
            S.op("pe", lambda h: h.transpose(out=out, in_=in_, identity=ident[:]), list(reads) + ["ident"], writes, inc)

        def ACT(out, in_, func, reads, writes, **kw):
            S.op("act", lambda h: h.activation(out=out, in_=in_, func=func, **kw), reads, writes)

        def STT(eng, out, in0, scalar, in1, op0, op1, reads, writes):
            S.op(eng, lambda h: h.scalar_tensor_tensor(out=out, in0=in0, scalar=scalar, in1=in1, op0=op0, op1=op1), reads, writes)

        def TT(eng, out, in0, in1, op, reads, writes):
            S.op(eng, lambda h: h.tensor_tensor(out=out, in0=in0, in1=in1, op=op), reads, writes)

        def CP(eng, out, in_, reads, writes):
            if eng == "act":
                ACT(out, in_, AF.Copy, reads, writes)
            else:
                S.op(eng, lambda h: h.tensor_copy(out=out, in_=in_), reads, writes)

        def MEMSET(eng, ap, val, writes):
            S.op(eng, lambda h: h.memset(ap, val), (), writes)

        MEMSET("pool", ident[:], 0.0, ["ident"])
        S.op("pool", lambda h: h.affine_select(out=ident[:], in_=ident[:], pattern=[[-1, 128]], compare_op=ALU.not_equal,
                                               fill=1.0, base=0, channel_multiplier=1), ["ident"], ["ident"])
        MEMSET("pool", ones_bf[:], 1.0, ["ones"])
        MEMSET("pool", tri[:], 1.0, ["tri"])
        S.op("pool", lambda h: h.affine_select(out=tri[:], in_=tri[:], pattern=[[-1, 128]], compare_op=ALU.is_gt,
                                               fill=0.0, base=0, channel_multiplier=1), ["tri"], ["tri"])
        MEMSET("pool", tri[64:128, 0:64], 0.0, ["tri"])
        MEMSET("pool", ci[:], 0.0, ["ci"])
        MEMSET("pool", ci[0:64, 0:1], 1.0, ["ci"])
        MEMSET("pool", ci[64:128, 1:2], 1.0, ["ci"])
        for l in range(L):
            MEMSET("pool", Sst[l][:], 0.0, [f"S{l}"])
        norm_srcs = []
        for l in range(L):
            norm_srcs += [ff1_norm[l:l + 1, :], mix_norm[l:l + 1, :], ff2_norm[l:l + 1, :]]
        norm_srcs.append(final_norm.rearrange("(o d) -> o d", o=1))
        for i, src in enumerate(norm_srcs):
            S.dma("sp", "cstA", lambda h, i=i, src=src: h.dma_start(out=nw[:, i * KC:(i + 1) * KC],
                                                                   in_=src.rearrange("o (c p) -> p (o c)", p=128)), (), ["nw"])
        S.lastw["nw"] = ("cstA", S.cnt["cstA"])
        def consts_B():
            for l in range(L):
                S.dma("sp", "cst", lambda h, l=l: h.dma_start(out=normo[:, l * 4:(l + 1) * 4],
                                                              in_=gla_norm_o[l:l + 1, :, :].rearrange("o h v -> v (o h)")), (), ["normo"])
                S.dma("sp", "cst", lambda h, l=l: h.dma_start(out=normv[l][:], in_=gmlp_norm_v[l:l + 1, :].to_broadcast([128, 512])), (), [f"normv{l}"])
                S.dma("sp", "cst", lambda h, l=l: h.dma_start(out=bgk[l][:], in_=gla_b_gk[l:l + 1, :].to_broadcast([128, 256])), (), [f"bgk{l}"])
                S.dma("pool", f"cstp{l}", lambda h, l=l: h.dma_start(out=wgk[l][:], in_=gla_w_gk2[l:l + 1, :, :].rearrange("o r n -> (o r) n")), (), [f"wgk{l}"])
                for g in range(8):
                    hh, j = g % 2, g // 2
                    S.dma("sp", "cst", lambda h, l=l, g=g, hh=hh, j=j: h.dma_start(
                        out=bsT[l][hh * 64:(hh + 1) * 64, j, :], in_=gmlp_b_s[l:l + 1, g:g + 1, :].rearrange("o g t -> (o g) t").to_broadcast([64, 128])), (), [f"bsT{l}"])
                wst = stg[l % 2][:].rearrange("p (g s) -> p g s", g=8)
                S.dma("sp", f"cws{l}", lambda h, l=l, wst=wst: h.dma_start(out=wst, in_=gmlp_w_s[l:l + 1, :, :, :].rearrange("o g t s -> t (o g) s")), (), [f"stg{l % 2}"])
                for g2 in range(2):
                    for g4 in range(4):
                        g = g2 * 4 + g4
                        TR(PS[g2][:, g4 * 128:(g4 + 1) * 128], wst[:, g, :], [f"stg{l % 2}"], [PSN[g2]], g4 == 3)
                    CP("dve", wsT[l][:, g2 * 4:(g2 + 1) * 4, :], PS[g2][:].rearrange("p (g t) -> p g t", g=4), [PSN[g2]], [f"wsT{l}"])
                MEMSET("dve", wsT[l][64:128, :, 0:64], 0.0, [f"wsT{l}"])

            for r in ["normo"] + [f"normv{l}" for l in range(L)] + [f"bgk{l}" for l in range(L)] + [f"bsT{l}" for l in range(L)]:
                S.lastw[r] = ("cst", S.cnt["cst"])
        ring_i = [0]

        def next_slot():
            i = ring_i[0] % NSLOT
            ring_i[0] += 1
            return i

        def wload(slot, dst, src):
            S.dma("pool", f"wq{slot}", lambda h: h.dma_start(out=dst, in_=src), (), [f"ring{slot}"])

        def load_x(ti):
            for blk in range(8):
                st = stg[blk % 2]
                sn = f"stg{blk % 2}"
                r0 = ti * T + blk * 128
                S.dma("sp", f"xq{blk % 2}", lambda h, st=st, r0=r0: h.dma_start(out=st[:], in_=x[r0:r0 + 128, :]), (), [sn])
                for g in range(2):
                    bank = 6 + g
                    for c4 in range(4):
                        c = g * 4 + c4
                        TR(PS[bank][:, c4 * 128:(c4 + 1) * 128], st[:, c * 128:(c + 1) * 128], [sn], [PSN[bank]], c4 == 3)
                    CP("dve" if g == 0 else "act", xT[:, g * 4:(g + 1) * 4, blk * 128:(blk + 1) * 128],
                       PS[bank][:].rearrange("p (c t) -> p c t", c=4), [PSN[bank]], [f"xT{blk // 4}"])

        def rsqrt_to(out, in_, scale, reads, writes, ln_buf=None, ln_name="at1"):
            lb = lnb[:] if ln_buf is None else ln_buf
            ACT(lb, in_, AF.Ln, reads, [ln_name], scale=scale, bias=EPS)
            ACT(out, lb, AF.Exp, [ln_name], writes, scale=-0.5)

        sq_i = [0]

        pending = []

        def flush_hooks(keep=0):
            while len(pending) > keep:
                pending.pop(0)()

        def stat_hook(hf, c, first, last, defer=False):
            cs = slice(hf * HT, (hf + 1) * HT)
            k = sq_i[0] % KC
            sq_i[0] += 1
            bank = 6 + hf
            ACT(sqy[:, k, :], xT[:, c, cs], AF.Square, [f"xT{hf}"], [f"sq{k}"])

            def part_b():
                MM(PS[bank][:], ones_bf[:], sqy[:, k, :], first, last, ["ones", f"sq{k}"], [PSN[bank]], last)
                if last:
                    rsqrt_to(rstd[hf][:], PS[bank][:], 1.0 / D, [PSN[bank]], [f"rstd{hf}"])
            if defer:
                pending.append(part_b)
            else:
                part_b()

        def rms_full(hf):
            for c in range(KC):
                stat_hook(hf, c, c == 0, c == KC - 1)

        def make_xw(hf, ni):
            cs = slice(hf * HT, (hf + 1) * HT)
            for c in range(KC):
                col = nw[:, ni * KC + c:ni * KC + c + 1]
                if c < 5:
                    S.op("dve", lambda h, c=c, col=col: h.tensor_scalar(out=xn[:, c, cs], in0=xT[:, c, cs], scalar1=col, scalar2=None, op0=ALU.mult),
                         [f"xT{hf}", "nw"], [f"xn{hf}_{c}"])
                else:
                    ACT(xn[:, c, cs], xT[:, c, cs], AF.Copy, [f"xT{hf}", "nw"], [f"xn{hf}_{c}"], scale=col)

        def norm_to_xn(hf, ni):
            cs = slice(hf * HT, (hf + 1) * HT)
            for c in range(KC):
                STT("dve", xn[:, c, cs], xT[:, c, cs], nw[:, ni * KC + c:ni * KC + c + 1], rstd[hf][:], ALU.mult, ALU.mult,
                    [f"xT{hf}", f"rstd{hf}", "nw"], [f"xn{hf}_{c}"])

        pair_i = [0]
        po_i = [0]

        def ffn(l, wi, wo, ni, skip_xw0=False, after_h0=None):
            wiv = wi[l:l + 1, :, :].rearrange("o (kc p) n -> p (o kc) n", p=128)
            wov = wo[l:l + 1, :, :].rearrange("o (f p) m -> p (o f) m", p=128)
            units = {}

            def p1_load(u):
                s = next_slot()
                wa = ring[s][:, 0:2048].rearrange("p (kc n) -> p kc n", kc=KC)
                wg = ring[s][:, 2048:4096].rearrange("p (kc n) -> p kc n", kc=KC)
                wload(s, wa, wiv[:, :, u * 256:(u + 1) * 256])
                wload(s, wg, wiv[:, :, FF + u * 256:FF + (u + 1) * 256])
                units[u] = (s, wa, wg)

            def p1_groups(u, hf):
                s, wa, wg = units[u]
                cs = slice(hf * HT, (hf + 1) * HT)
                for f2 in range(2):
                    f = 2 * u + f2
                    pi = pair_i[0] % 2
                    pair_i[0] += 1
                    ba, bg = 2 * pi, 2 * pi + 1
                    for kc in range(KC):
                        MM(PS[ba][:], wa[:, kc, f2 * 128:(f2 + 1) * 128], xn[:, kc, cs], kc == 0, kc == KC - 1,
                           [f"ring{s}", f"xn{hf}_{kc}"], [PSN[ba]], kc == KC - 1)
                    for kc in range(KC):
                        MM(PS[bg][:], wg[:, kc, f2 * 128:(f2 + 1) * 128], xn[:, kc, cs], kc == 0, kc == KC - 1,
                           [f"ring{s}", f"xn{hf}_{kc}"], [PSN[bg]], kc == KC - 1)
                    TT("dve", sg[pi][:], PS[bg][:], rstd[hf][:], ALU.mult, [PSN[bg], f"rstd{hf}"], [f"sg{pi}"])
                    ACT(sg[pi][:], sg[pi][:], AF.Silu, [f"sg{pi}"], [f"sg{pi}"])
                    TT("dve", at[pi][:], PS[ba][:], rstd[hf][:], ALU.mult, [PSN[ba], f"rstd{hf}"], [f"at{pi}"])
                    TT("dve", h_all[:, f, cs], at[pi][:], sg[pi][:], ALU.mult, [f"at{pi}", f"sg{pi}"], [f"h{hf}"])

            if not skip_xw0:
                make_xw(0, ni)
            fence()
            make_xw(1, ni)
            for u in range(FC // 2):
                p1_load(u)
                for hf in range(2):
                    p1_groups(u, hf)
            w2s = {}

            def p2_load(m):
                s = next_slot()
                w2 = ring[s][:, 0:FC * 128].rearrange("p (f n) -> p f n", f=FC)
                wload(s, w2, wov[:, :, m * 128:(m + 1) * 128])
                w2s[m] = (s, w2)

            def p2_group(m, hf):
                s, w2 = w2s[m]
                cs = slice(hf * HT, (hf + 1) * HT)
                bo = 4 + po_i[0] % 2
                po_i[0] += 1
                for f in range(FC):
                    MM(PS[bo][:], w2[:, f, :], h_all[:, f, cs], f == 0, f == FC - 1, [f"ring{s}", f"h{hf}"], [PSN[bo]], f == FC - 1)
                flush_hooks()
                STT("dve", xT[:, m, cs], PS[bo][:], 0.5, xT[:, m, cs], ALU.mult, ALU.add, [PSN[bo], f"xT{hf}"], [f"xT{hf}"])
                stat_hook(hf, m, m == 0, m == KC - 1, defer=True)

            NL = 3
            for m in range(KC - NL):
                p2_load(m)
                for hf in range(2):
                    p2_group(m, hf)
            for m in range(KC - NL, KC):
                p2_load(m)
            for m in range(KC - NL, KC):
                p2_group(m, 0)
            p2_group(KC - NL, 1)
            if after_h0 is not None:
                after_h0()
            for m in range(KC - NL + 1, KC):
                p2_group(m, 1)
            flush_hooks()

        def mixer(l, hf, skip_norm=False, after_onorm=None):
            cs = slice(hf * HT, (hf + 1) * HT)
            XNS = [f"xn{hf}_{c}" for c in range(KC)]
            if not skip_norm:
                norm_to_xn(hf, 3 * l + 1)
            fence()
            wv = w_in[l:l + 1, :, :].rearrange("o (kc p) n -> p (o kc) n", p=128)
            wov = w_out[l:l + 1, :, :].rearrange("o (j p) m -> p (o j) m", p=128)
            SN = f"S{l}"

            def unit(c0, ncols):
                s = next_slot()
                v = ring[s][:, 0:KC * ncols].rearrange("p (kc n) -> p kc n", kc=KC)
                wload(s, v, wv[:, :, c0:c0 + ncols])
                return s, v

            def fm_group(s, w, c0, b):
                for kc in range(KC):
                    MM(PS[b][:], w[:, kc, c0:c0 + 128], xn[:, kc, cs], kc == 0, kc == KC - 1, [f"ring{s}", XNS[kc]], [PSN[b]], kc == KC - 1)

            def tm_group(s, w, c0, n, blk, b):
                for kc in range(KC):
                    MM(PS[b][:, 0:n], xn[:, kc, hf * HT + blk * 128:hf * HT + (blk + 1) * 128], w[:, kc, c0:c0 + n], kc == 0, kc == KC - 1,
                       [f"ring{s}", XNS[kc]], [PSN[b]], kc == KC - 1)

            s, wr = unit(2560, 16)
            for kc in range(KC):
                MM(PS[7][0:16, :], wr[:, kc, :], xn[:, kc, cs], kc == 0, kc == KC - 1, [f"ring{s}", XNS[kc]], [PSN[7]], kc == KC - 1)
            CP("act", rT[0:16, :], PS[7][0:16, :], [PSN[7]], ["rT"])
            sqk, wqk = unit(1024, 512)
            for c in range(2):
                fm_group(sqk, wqk, c * 128, c)
                ACT(qT[:, c, :], PS[c][:], AF.Copy, [PSN[c]], ["qT"], scale=0.125)
            for blk in range(4):
                b = 2 + blk // 2
                co = (blk % 2) * 256
                MM(PS[b][:, co:co + 256], rT[0:16, blk * 128:(blk + 1) * 128], wgk[l][:], True, True, ["rT", f"wgk{l}"], [PSN[b]], True)
            for blk in range(4):
                b = 2 + blk // 2
                co = (blk % 2) * 256
                i2 = blk % 2
                TT("dve", xgb[i2][:], PS[b][:, co:co + 256], bgk[l][:], ALU.add, [PSN[b], f"bgk{l}"], [f"xg{i2}"])
                ACT(xgb[i2][:], xgb[i2][:], AF.Exp, [f"xg{i2}"], [f"xg{i2}"], scale=-1.0)
                ACT(spa[:, blk, :], xgb[i2][:], AF.Ln, [f"xg{i2}"], [f"spa{blk}"], bias=1.0)
            s, wvb = unit(1536, 512)
            for blk in range(4):
                b = 4 + blk % 2
                tm_group(s, wvb, 0, 512, blk, b)
                CP("act" if blk % 2 else "dve", vtok[:, blk, :], PS[b][:], [PSN[b]], [f"vtok{blk}"])
            for blk in range(4):
                co = (blk // 2) * 256
                bE = 2 + blk % 2
                MM(PS[bE][:, co:co + 256], tri[:], spa[:, blk, :], True, True, ["tri", f"spa{blk}"], [PSN[bE]], True)
                for j in range(2):
                    MM(PS[1][:, blk * 4 + 2 * j:blk * 4 + 2 * j + 2], spa[:, blk, j * 128:(j + 1) * 128], ci[:], True, True,
                       ["ci", f"spa{blk}"], [PSN[1]], j == 1)
                ACT(wdec[:, blk, :], PS[bE][:, co:co + 256], AF.Exp, [PSN[bE]], ["wdec"], scale=-1.0 / 16)
            ACT(dec[:], PS[1][:, 0:16], AF.Exp, [PSN[1]], ["dec"], scale=-1.0 / 16)
            s, wva = unit(512, 512)
            for blk in range(4):
                b = 4 + blk
                tm_group(s, wva, 0, 512, blk, b)
                ACT(junk[:], PS[b][:], AF.Square, [PSN[b]], ["junk", "vst"], accum_out=vst[:, blk:blk + 1])
                rsqrt_to(vst[:, 4 + blk:5 + blk], vst[:, blk:blk + 1], 1.0 / 512, ["vst"], ["vst"], ln_buf=vst[:, blk:blk + 1], ln_name="vst")
                STT("dve", vn[:, blk, :], PS[b][:], vst[:, 4 + blk:5 + blk], normv[l][:], ALU.mult, ALU.mult,
                    [PSN[b], "vst", f"normv{l}"], ["vn"])
            for blk in range(4):
                b = blk % 2
                tm_group(sqk, wqk, 256, 256, blk, b)
                TT("dve", kdec[:, blk, :], PS[b][:, 0:256], wdec[:, blk, :], ALU.mult, [PSN[b], "wdec"], [f"kdec{blk}"])

            fill = []
            su, wu = unit(0, 512)

            def f_u(c):
                b = 6 + c % 2
                fm_group(su, wu, c * 128, b)
                CP("act", uT[:, c, :], PS[b][:], [PSN[b]], ["uT"])

            def f_z(j):
                b = 6 + j % 2
                for blk in range(4):
                    for hh in range(2):
                        g = 2 * j + hh
                        MM(PS[b][hh * 64:(hh + 1) * 64, blk * 128:(blk + 1) * 128], vn[:, blk, g * 64:(g + 1) * 64], wsT[l][:, g, :], True, True,
                           ["vn", f"wsT{l}"], [PSN[b]], (blk == 3 and hh == 1), tile_position=(0, hh * 64))
                i2 = j % 2
                TT("dve", tmp[i2][:].rearrange("p (b t) -> p b t", b=4), PS[b][:].rearrange("p (b t) -> p b t", b=4),
                   bsT[l][:, j:j + 1, :].to_broadcast([128, 4, 128]), ALU.add, [PSN[b], f"bsT{l}"], [f"tmp{i2}"])
                TT("dve", yT[:, j, :], tmp[i2][:], uT[:, j, :], ALU.mult, [f"tmp{i2}", "uT"], [f"yT{j}"])

            gunit = []

            def f_g(c):
                if not gunit:
                    gunit.append(unit(2048, 512))
                sg_, wgg = gunit[0]
                b = 6 + c % 2
                fm_group(sg_, wgg, c * 128, b)
                ACT(sgT[:, c, :], PS[b][:], AF.Silu, [PSN[b]], ["sgT"])

            fill = [lambda c=c: f_u(c) for c in range(4)] + [lambda j=j: f_z(j) for j in range(4)] + [lambda c=c: f_g(c) for c in range(4)]

            def upd(c):
                blk, ch = c // 2, c % 2
                rows = slice(ch * 64, (ch + 1) * 64)
                b = 4 + c % 2
                for j in range(2):
                    MM(PS[b][:, j * 256:(j + 1) * 256], kdec[rows, blk, j * 128:(j + 1) * 128], vtok[rows, blk, j * 256:(j + 1) * 256], True, True,
                       [f"kdec{blk}", f"vtok{blk}"], [PSN[b]], j == 1)

            def state(c):
                blk, ch = c // 2, c % 2
                b = 4 + c % 2
                for j in range(2):
                    for hh in range(2):
                        pr = slice(hh * 64, (hh + 1) * 64)
                        col = blk * 4 + 2 * j + ch
                        STT("dve", Sst[l][pr, j, :], Sst[l][pr, j, :], dec[pr, col:col + 1],
                            PS[b][pr, j * 256 + hh * 128:j * 256 + (hh + 1) * 128], ALU.mult, ALU.add, [SN, "dec", PSN[b]], [SN])
                CP("act", Sbf[c % 2][:], Sst[l][:], [SN], [f"Sbf{c % 2}"])

            def omm(c):
                for hd in range(4):
                    j, hh = hd // 2, hd % 2
                    pr = slice(hh * 64, (hh + 1) * 64)
                    MM(PS[hd][:, c * 64:(c + 1) * 64], Sbf[c % 2][pr, j, :], qT[pr, j, c * 64:(c + 1) * 64], True, True,
                       [f"Sbf{c % 2}", "qT"], [PSN[hd]], hd == 3)

            upd(0)
            upd(1)
            for c in range(8):
                state(c)
                nf = 2 if c < 3 else 1
                for _ in range(nf):
                    if len(fill) > 1:
                        fill.pop(0)()
                if c + 2 < 8:
                    upd(c + 2)
                omm(c)
            osq4 = [(osq[0], "osq0"), (osq[1], "osq1"), (junk, "junk"), (rT, "rT")]
            for hd in range(4):
                ACT(osq4[hd][0][:], PS[hd][:], AF.Square, [PSN[hd]], [osq4[hd][1]])
            while fill:
                fill.pop(0)()
            for hd in range(4):
                MM(PS[4 + hd][:], ones_bf[:], osq4[hd][0][:], True, True, ["ones", osq4[hd][1]], [PSN[4 + hd]], True)
            for hd in range(4):
                i2 = hd % 2
                rsqrt_to(ornd[i2][:], PS[4 + hd][:], 1.0 / 128, [PSN[4 + hd]], [f"sg{i2}"], ln_buf=lnb2[i2][:], ln_name=f"at{i2}")
                STT("dve", tmp[i2][:], PS[hd][:], normo[:, l * 4 + hd:l * 4 + hd + 1], ornd[i2][:], ALU.mult, ALU.mult,
                    [PSN[hd], "normo", f"sg{i2}"], [f"tmp{i2}"])
                TT("dve", yT[:, 4 + hd, :], tmp[i2][:], sgT[:, hd, :], ALU.mult, [f"tmp{i2}", "sgT"], [f"yT{4 + hd}"])
            if after_onorm is not None:
                after_onorm()
            for mu in range(2):
                s = next_slot()
                w2 = ring[s][:, 0:KC * 512].rearrange("p (j n) -> p j n", j=KC)
                wload(s, w2, wov[:, :, mu * 512:(mu + 1) * 512])
                for m4 in range(4):
                    m = mu * 4 + m4
                    b = 4 + m % 2
                    for j in range(KC):
                        MM(PS[b][:], w2[:, j, m4 * 128:(m4 + 1) * 128], yT[:, j, :], j == 0, j == KC - 1, [f"ring{s}", f"yT{j}"], [PSN[b]], j == KC - 1)
                    flush_hooks()
                    TT("dve", xT[:, m, cs], PS[b][:], xT[:, m, cs], ALU.add, [PSN[b], f"xT{hf}"], [f"xT{hf}"])
                    stat_hook(hf, m, m == 0, m == KC - 1, defer=True)
            flush_hooks()

        out_toks = []

        def final(ti):
            ni = 3 * L
            for hf in range(2):
                cs = slice(hf * HT, (hf + 1) * HT)
                fence()
                for c in range(KC):
                    STT("dve", yf[:, c, :], xT[:, c, cs], nw[:, ni * KC + c:ni * KC + c + 1], rstd[hf][:], ALU.mult, ALU.mult,
                        [f"xT{hf}", f"rstd{hf}", "nw"], ["yf"])
                for blk in range(4):
                    st = stg[blk % 2]
                    sn = f"stg{blk % 2}"
                    for g in range(2):
                        bank = 6 + g
                        for c4 in range(4):
                            c = g * 4 + c4
                            TR(PS[bank][:, c4 * 128:(c4 + 1) * 128], yf[:, c, blk * 128:(blk + 1) * 128], ["yf"], [PSN[bank]], c4 == 3)
                        CP("dve" if g == 0 else "act", st[:, g * 512:(g + 1) * 512], PS[bank][:], [PSN[bank]], [sn])
                    r0 = ti * T + hf * HT + blk * 128
                    out_toks.append(S.dma("sp", f"oq{blk % 2}", lambda h, st=st, r0=r0: h.dma_start(out=out[r0:r0 + 128, :], in_=st[:]), [sn], ()))

        for ti in range(n_tiles):
            load_x(ti)
            for hf in range(2):
                rms_full(hf)
            for l in range(n_layers):
                ffn(l, ff1_w_in, ff1_w_out, 3 * l, skip_xw0=(l > 0),
                    after_h0=(None if (ti == 0 and l == 0) else (lambda l=l: norm_to_xn(0, 3 * l + 1))))
                if ti == 0 and l == 0:
                    consts_B()
                mixer(l, 0, skip_norm=not (ti == 0 and l == 0), after_onorm=lambda l=l: norm_to_xn(1, 3 * l + 1))
                mixer(l, 1, skip_norm=True, after_onorm=lambda l=l: make_xw(0, 3 * l + 2))
                ffn(l, ff2_w_in, ff2_w_out, 3 * l + 2, skip_xw0=True,
                    after_h0=((lambda l=l: make_xw(0, 3 * (l + 1))) if l + 1 < n_layers else None))
            final(ti)
        last = {}
        for s, v in out_toks:
            last[s] = max(last.get(s, 0), v)
        S.wait_all("sp", list(last.items()))
        with nc.allow_non_contiguous_dma(reason="tiny constant loads"):
            S.run()
    return nc


_NC_CACHE = {}


def kernel(**inputs):
    if "nc" not in _NC_CACHE:
        _NC_CACHE["nc"] = build()
    nc = _NC_CACHE["nc"]
    x = np.ascontiguousarray(np.asarray(inputs["x"], dtype=np.float32))
    shared = {k: np.ascontiguousarray(np.asarray(v, dtype=np.float32)) for k, v in inputs.items() if k != "x"}
    in_maps = []
    for b in range(N_CORES):
        m = dict(shared)
        m["x"] = x[b]
        in_maps.append(m)
    res = run_bass_kernel_spmd(nc, in_maps, core_ids=list(range(N_CORES)))
    return np.stack([np.asarray(r["out"], dtype=np.float32) for r in res.results], axis=0)
```

```python
import numpy as np
from contextlib import ExitStack
import concourse.bass as bass
import concourse.mybir as mybir
from concourse.bass_utils import run_bass_kernel_spmd

F32 = mybir.dt.float32
BF16 = mybir.dt.bfloat16
AF = mybir.ActivationFunctionType
ALU = mybir.AluOpType

L = 2
D = 1024
KC = 8
SEQ = 4096
T = 1024
NT = SEQ // T
HT = 512
FF = 2816
FC = 22
IN_COLS = 2576
EPS = 1e-6
NSLOT = 7
SLOT = 4096
N_CORES = 8


class Sched:
    ENG = ("pe", "act", "dve", "pool", "sp")

    def __init__(self, nc, es):
        self.nc = nc
        self.es = es
        self.q = {e: [] for e in self.ENG}
        self.sems = {}
        self.cnt = {}
        self.waited = {e: {} for e in self.ENG}
        self.lastw = {}
        self.readers = {}
        for e in self.ENG:
            self._sem(e)

    def _sem(self, name):
        if name not in self.sems:
            self.sems[name] = self.es.enter_context(self.nc.semaphore("s_" + name))
            self.cnt[name] = 0
        return self.sems[name]

    def _deps(self, eng, reads, writes):
        deps = {}

        def add(tok):
            if tok is None:
                return
            s, v = tok
            if s == eng and eng == "pe":
                return
            if deps.get(s, 0) < v:
                deps[s] = v
        for r in reads:
            add(self.lastw.get(r))
        for w in writes:
            add(self.lastw.get(w))
            for t in self.readers.get(w, ()):
                add(t)
        for s, v in deps.items():
            if self.waited[eng].get(s, 0) < v:
                self.waited[eng][s] = v
                sem = self.sems[s]
                self.q[eng].append(lambda h, sem=sem, v=v: h.wait_ge(sem, v))

    def _post(self, tok, reads, writes):
        for r in reads:
            lst = self.readers.setdefault(r, [])
            if not lst or lst[-1] != tok:
                lst.append(tok)
        for w in writes:
            self.lastw[w] = tok
            self.readers[w] = []

    def op(self, eng, fn, reads=(), writes=(), inc=True):
        self._deps(eng, reads, writes)
        sem = self.sems[eng]
        if inc:
            self.cnt[eng] += 1
            tok = (eng, self.cnt[eng])
            self.q[eng].append(lambda h, fn=fn, sem=sem: fn(h).then_inc(sem, 1))
        else:
            tok = (eng, self.cnt[eng] + 1)
            self.q[eng].append(lambda h, fn=fn: fn(h))
        self._post(tok, reads, writes)
        return tok

    def dma(self, eng, semname, fn, reads=(), writes=()):
        self._deps(eng, reads, writes)
        sem = self._sem(semname)
        self.cnt[semname] += 16
        tok = (semname, self.cnt[semname])
        self.q[eng].append(lambda h, fn=fn, sem=sem: fn(h).then_inc(sem, 16))
        self._post(tok, reads, writes)
        return tok

    def wait_all(self, eng, toks):
        for s, v in toks:
            if self.waited[eng].get(s, 0) < v:
                self.waited[eng][s] = v
                sem = self.sems[s]
                self.q[eng].append(lambda h, sem=sem, v=v: h.wait_ge(sem, v))

    def run(self):
        nc = self.nc
        with nc.Block() as block:
            @block.tensor
            def _(h):
                for f in self.q["pe"]:
                    f(h)

            @block.scalar
            def _(h):
                for f in self.q["act"]:
                    f(h)

            @block.vector
            def _(h):
                for f in self.q["dve"]:
                    f(h)

            @block.gpsimd
            def _(h):
                for f in self.q["pool"]:
                    f(h)

            @block.sync
            def _(h):
                for f in self.q["sp"]:
                    f(h)


def build(n_tiles=NT, n_layers=L):
    nc = bass.Bass("TRN2", target_bir_lowering=False)

    def din(name, shape):
        return nc.dram_tensor(name, shape, F32, kind="ExternalInput").ap()
    x = din("x", [SEQ, D])
    out = nc.dram_tensor("out", [SEQ, D], F32, kind="ExternalOutput").ap()
    ff1_norm = din("ff1_norm", [L, D])
    ff1_w_in = din("ff1_w_in", [L, D, 2 * FF])
    ff1_w_out = din("ff1_w_out", [L, FF, D])
    mix_norm = din("mix_norm", [L, D])
    w_in = din("w_in", [L, D, IN_COLS])
    gmlp_norm_v = din("gmlp_norm_v", [L, 512])
    gmlp_w_s = din("gmlp_w_s", [L, 8, 128, 128])
    gmlp_b_s = din("gmlp_b_s", [L, 8, 128])
    gla_w_gk2 = din("gla_w_gk2", [L, 16, 256])
    gla_b_gk = din("gla_b_gk", [L, 256])
    gla_norm_o = din("gla_norm_o", [L, 4, 128])
    w_out = din("w_out", [L, D, D])
    ff2_norm = din("ff2_norm", [L, D])
    ff2_w_in = din("ff2_w_in", [L, D, 2 * FF])
    ff2_w_out = din("ff2_w_out", [L, FF, D])
    final_norm = din("final_norm", [D])

    es = ExitStack()
    with es:
        S = Sched(nc, es)

        def sb(name, shape, dt):
            return es.enter_context(nc.sbuf_tensor(name, shape, dt))
        xT = sb("xT", [128, KC, T], F32)
        xn = sb("xn", [128, KC, T], BF16)
        hbuf = sb("hbuf", [128, 11264], F32)
        sqy = sb("sqy", [128, KC, HT], BF16)
        rstd = [sb(f"rstd{i}", [128, HT], F32) for i in range(2)]
        spa = sb("spa", [128, 4, 256], F32)
        sg = [sb(f"sg{i}", [128, HT], F32) for i in range(2)]
        at = [sb(f"at{i}", [128, HT], F32) for i in range(2)]
        ornd = sg
        lnb2 = at
        lnb = at[1]
        ring = [sb(f"ring{i}", [128, SLOT], BF16) for i in range(NSLOT)]
        stg = [sb(f"stg{i}", [128, D], F32) for i in range(2)]
        ident = sb("ident", [128, 128], F32)
        ones_bf = sb("ones_bf", [128, 128], BF16)
        tri = sb("tri", [128, 128], F32)
        ci = sb("ci", [128, 2], F32)
        dummy = sb("fdummy", [128, 8], F32)
        NNW = 3 * L + 1
        nw = sb("nw", [128, NNW * KC], F32)
        normo = sb("normo", [128, L * 4], F32)
        wsT = [sb(f"wsT{l}", [128, 8, 128], BF16) for l in range(L)]
        bsT = [sb(f"bsT{l}", [128, 4, 128], F32) for l in range(L)]
        normv = [sb(f"normv{l}", [128, 512], F32) for l in range(L)]
        bgk = [sb(f"bgk{l}", [128, 256], F32) for l in range(L)]
        wgk = [sb(f"wgk{l}", [16, 256], BF16) for l in range(L)]
        Sst = [sb(f"Sst{l}", [128, 2, 128], F32) for l in range(L)]
        Sbf = [sb(f"Sbf{i}", [128, 2, 128], BF16) for i in range(2)]
        dec = sb("dec", [128, 16], F32)
        vst = sb("vst", [128, 8], F32)
        PS = [es.enter_context(nc.psum_tensor(f"ps{i}", [128, 512], F32)) for i in range(8)]
        PSN = [f"ps{i}" for i in range(8)]

        def hv(off, words, dt):
            v = hbuf[:, off:off + words]
            return v.bitcast(dt) if dt is not F32 else v
        h_all = hv(0, 11264, BF16).rearrange("p (f t) -> p f t", f=FC)
        o = 0
        uT = hv(o, 2048, F32).rearrange("p (c t) -> p c t", c=4); o += 2048
        sgT = hv(o, 2048, F32).rearrange("p (c t) -> p c t", c=4); o += 2048
        vn = hv(o, 1024, BF16).rearrange("p (b n) -> p b n", b=4); o += 1024
        qT = hv(o, 512, BF16).rearrange("p (c t) -> p c t", c=2); o += 512
        wdec = hv(o, 1024, F32).rearrange("p (b n) -> p b n", b=4); o += 1024
        kdec = hv(o, 512, BF16).rearrange("p (b n) -> p b n", b=4); o += 512
        vtok = hv(o, 1024, BF16).rearrange("p (b n) -> p b n", b=4); o += 1024
        tmp = []
        for i in range(2):
            tmp.append(hv(o, 512, F32)); o += 512
        osq = []
        for i in range(2):
            osq.append(hv(o, 256, BF16)); o += 256
        spb = []
        for i in range(2):
            spb.append(hv(o, 256, F32)); o += 256
        xgb = []
        for i in range(2):
            xgb.append(hv(o, 256, F32)); o += 256
        rT = hv(o, 256, BF16); o += 256
        junk = hv(o, 256, BF16); o += 256
        assert o <= 11264 - 2048 + 2048
        yf = hv(0, 4096, F32).rearrange("p (c t) -> p c t", c=KC)
        yT = sb("yT", [128, KC, HT], BF16)
        HGRP = ["h0", "h1", "uT", "sgT", "vn", "qT", "wdec", "kdec0", "kdec1", "kdec2", "kdec3", "vtok0", "vtok1", "vtok2", "vtok3", "tmp0", "tmp1", "osq0", "osq1",
                "sp0", "sp1", "xg0", "xg1", "rT", "junk", "yf"]

        def fence():
            S.op("dve", lambda h: h.memset(dummy[:], 0.0), writes=HGRP)

        def MM(out, lhsT, rhs, start, stop, reads, writes, inc, **kw):
            S.op("pe", lambda h: h.matmul(out, lhsT=lhsT, rhs=rhs, start=start, stop=stop, **kw), reads, writes, inc)

        def TR(out, in_, reads, writes, inc):
            S.op("pe", lambda h: h.transpose(out=out, in_=in_, identity=ident[:]), list(reads) + ["ident"], writes, inc)

        def ACT(out, in_, func, reads, writes, **kw):
            S.op("act", lambda h: h.activation(out=out, in_=in_, func=func, **kw), reads, writes)

        def STT(eng, out, in0, scalar, in1, op0, op1, reads, writes):
            S.op(eng, lambda h: h.scalar_tensor_tensor(out=out, in0=in0, scalar=scalar, in1=in1, op0=op0, op1=op1), reads, writes)

        def TT(eng, out, in0, in1, op, reads, writes):
            S.op(eng, lambda h: h.tensor_tensor(out=out, in0=in0, in1=in1, op=op), reads, writes)

        def CP(eng, out, in_, reads, writes):
            if eng == "act":
                ACT(out, in_, AF.Copy, reads, writes)
            else:
                S.op(eng, lambda h: h.tensor_copy(out=out, in_=in_), reads, writes)

        def MEMSET(eng, ap, val, writes):
            S.op(eng, lambda h: h.memset(ap, val), (), writes)

        MEMSET("pool", ident[:], 0.0, ["ident"])
        S.op("pool", lambda h: h.affine_select(out=ident[:], in_=ident[:], pattern=[[-1, 128]], compare_op=ALU.not_equal,
                                               fill=1.0, base=0, channel_multiplier=1), ["ident"], ["ident"])
        MEMSET("pool", ones_bf[:], 1.0, ["ones"])
        MEMSET("pool", tri[:], 1.0, ["tri"])
        S.op("pool", lambda h: h.affine_select(out=tri[:], in_=tri[:], pattern=[[-1, 128]], compare_op=ALU.is_gt,
                                               fill=0.0, base=0, channel_multiplier=1), ["tri"], ["tri"])
        MEMSET("pool", tri[64:128, 0:64], 0.0, ["tri"])
        MEMSET("pool", ci[:], 0.0, ["ci"])
        MEMSET("pool", ci[0:64, 0:1], 1.0, ["ci"])
        MEMSET("pool", ci[64:128, 1:2], 1.0, ["ci"])
        for l in range(L):
            MEMSET("pool", Sst[l][:], 0.0, [f"S{l}"])
        norm_srcs = []
        for l in range(L):
            norm_srcs += [ff1_norm[l:l + 1, :], mix_norm[l:l + 1, :], ff2_norm[l:l + 1, :]]
        norm_srcs.append(final_norm.rearrange("(o d) -> o d", o=1))
        for i, src in enumerate(norm_srcs):
            S.dma("sp", "cstA", lambda h, i=i, src=src: h.dma_start(out=nw[:, i * KC:(i + 1) * KC],
                                                                   in_=src.rearrange("o (c p) -> p (o c)", p=128)), (), ["nw"])
        S.lastw["nw"] = ("cstA", S.cnt["cstA"])
        def consts_B():
            for l in range(L):
                S.dma("sp", "cst", lambda h, l=l: h.dma_start(out=normo[:, l * 4:(l + 1) * 4],
                                                              in_=gla_norm_o[l:l + 1, :, :].rearrange("o h v -> v (o h)")), (), ["normo"])
                S.dma("sp", "cst", lambda h, l=l: h.dma_start(out=normv[l][:], in_=gmlp_norm_v[l:l + 1, :].to_broadcast([128, 512])), (), [f"normv{l}"])
                S.dma("sp", "cst", lambda h, l=l: h.dma_start(out=bgk[l][:], in_=gla_b_gk[l:l + 1, :].to_broadcast([128, 256])), (), [f"bgk{l}"])
                S.dma("pool", f"cstp{l}", lambda h, l=l: h.dma_start(out=wgk[l][:], in_=gla_w_gk2[l:l + 1, :, :].rearrange("o r n -> (o r) n")), (), [f"wgk{l}"])
                for g in range(8):
                    hh, j = g % 2, g // 2
                    S.dma("sp", "cst", lambda h, l=l, g=g, hh=hh, j=j: h.dma_start(
                        out=bsT[l][hh * 64:(hh + 1) * 64, j, :], in_=gmlp_b_s[l:l + 1, g:g + 1, :].rearrange("o g t -> (o g) t").to_broadcast([64, 128])), (), [f"bsT{l}"])
                wst = stg[l % 2][:].rearrange("p (g s) -> p g s", g=8)
                S.dma("sp", f"cws{l}", lambda h, l=l, wst=wst: h.dma_start(out=wst, in_=gmlp_w_s[l:l + 1, :, :, :].rearrange("o g t s -> t (o g) s")), (), [f"stg{l % 2}"])
                for g2 in range(2):
                    for g4 in range(4):
                        g = g2 * 4 + g4
                        TR(PS[g2][:, g4 * 128:(g4 + 1) * 128], wst[:, g, :], [f"stg{l % 2}"], [PSN[g2]], g4 == 3)
                    CP("dve", wsT[l][:, g2 * 4:(g2 + 1) * 4, :], PS[g2][:].rearrange("p (g t) -> p g t", g=4), [PSN[g2]], [f"wsT{l}"])
                MEMSET("dve", wsT[l][64:128, :, 0:64], 0.0, [f"wsT{l}"])

            for r in ["normo"] + [f"normv{l}" for l in range(L)] + [f"bgk{l}" for l in range(L)] + [f"bsT{l}" for l in range(L)]:
                S.lastw[r] = ("cst", S.cnt["cst"])
        ring_i = [0]

        def next_slot():
            i = ring_i[0] % NSLOT
            ring_i[0] += 1
            return i

        def wload(slot, dst, src):
            S.dma("pool", f"wq{slot}", lambda h: h.dma_start(out=dst, in_=src), (), [f"ring{slot}"])

        def load_x_half(ti, hf, banks=(6, 7)):
            for blk in range(4 * hf, 4 * hf + 4):
                st = stg[blk % 2]
                sn = f"stg{blk % 2}"
                r0 = ti * T + blk * 128
                S.dma("sp", f"xq{blk % 2}", lambda h, st=st, r0=r0: h.dma_start(out=st[:], in_=x[r0:r0 + 128, :]), (), [sn])
                for g in range(2):
                    bank = banks[g]
                    for c4 in range(4):
                        c = g * 4 + c4
                        TR(PS[bank][:, c4 * 128:(c4 + 1) * 128], st[:, c * 128:(c + 1) * 128], [sn], [PSN[bank]], c4 == 3)
                    CP("dve" if g == 0 else "act", xT[:, g * 4:(g + 1) * 4, blk * 128:(blk + 1) * 128],
                       PS[bank][:].rearrange("p (c t) -> p c t", c=4), [PSN[bank]], [f"xT{blk // 4}"])

        def rsqrt_to(out, in_, scale, reads, writes, ln_buf=None, ln_name="at1"):
            lb = lnb[:] if ln_buf is None else ln_buf
            ACT(lb, in_, AF.Ln, reads, [ln_name], scale=scale, bias=EPS)
            ACT(out, lb, AF.Exp, [ln_name], writes, scale=-0.5)

        sq_i = [0]

        pending = []

        def flush_hooks(keep=0):
            while len(pending) > keep:
                pending.pop(0)()

        def stat_hook(hf, c, first, last, defer=False):
            cs = slice(hf * HT, (hf + 1) * HT)
            k = sq_i[0] % KC
            sq_i[0] += 1
            bank = 6 + hf
            ACT(sqy[:, k, :], xT[:, c, cs], AF.Square, [f"xT{hf}"], [f"sq{k}"])

            def part_b():
                MM(PS[bank][:], ones_bf[:], sqy[:, k, :], first, last, ["ones", f"sq{k}"], [PSN[bank]], last)
                if last:
                    rsqrt_to(rstd[hf][:], PS[bank][:], 1.0 / D, [PSN[bank]], [f"rstd{hf}"])
            if defer:
                pending.append(part_b)
            else:
                part_b()

        def rms_full(hf):
            for c in range(KC):
                stat_hook(hf, c, c == 0, c == KC - 1)

        def make_xw(hf, ni):
            cs = slice(hf * HT, (hf + 1) * HT)
            for c in range(KC):
                col = nw[:, ni * KC + c:ni * KC + c + 1]
                if c < 5:
                    S.op("dve", lambda h, c=c, col=col: h.tensor_scalar(out=xn[:, c, cs], in0=xT[:, c, cs], scalar1=col, scalar2=None, op0=ALU.mult),
                         [f"xT{hf}", "nw"], [f"xn{hf}_{c}"])
                else:
                    ACT(xn[:, c, cs], xT[:, c, cs], AF.Copy, [f"xT{hf}", "nw"], [f"xn{hf}_{c}"], scale=col)

        def norm_to_xn(hf, ni):
            cs = slice(hf * HT, (hf + 1) * HT)
            for c in range(KC):
                STT("dve", xn[:, c, cs], xT[:, c, cs], nw[:, ni * KC + c:ni * KC + c + 1], rstd[hf][:], ALU.mult, ALU.mult,
                    [f"xT{hf}", f"rstd{hf}", "nw"], [f"xn{hf}_{c}"])

        pair_i = [0]
        po_i = [0]

        def ffn(l, wi, wo, ni, skip_xw0=False, after_h0=None, tail_cbs=None):
            wiv = wi[l:l + 1, :, :].rearrange("o (kc p) n -> p (o kc) n", p=128)
            wov = wo[l:l + 1, :, :].rearrange("o (f p) m -> p (o f) m", p=128)
            units = {}

            def p1_load(u):
                s = next_slot()
                wa = ring[s][:, 0:2048].rearrange("p (kc n) -> p kc n", kc=KC)
                wg = ring[s][:, 2048:4096].rearrange("p (kc n) -> p kc n", kc=KC)
                wload(s, wa, wiv[:, :, u * 256:(u + 1) * 256])
                wload(s, wg, wiv[:, :, FF + u * 256:FF + (u + 1) * 256])
                units[u] = (s, wa, wg)

            def p1_groups(u, hf):
                s, wa, wg = units[u]
                cs = slice(hf * HT, (hf + 1) * HT)
                for f2 in range(2):
                    f = 2 * u + f2
                    pi = pair_i[0] % 2
                    pair_i[0] += 1
                    ba, bg = 2 * pi, 2 * pi + 1
                    for kc in range(KC):
                        MM(PS[ba][:], wa[:, kc, f2 * 128:(f2 + 1) * 128], xn[:, kc, cs], kc == 0, kc == KC - 1,
                           [f"ring{s}", f"xn{hf}_{kc}"], [PSN[ba]], kc == KC - 1)
                    for kc in range(KC):
                        MM(PS[bg][:], wg[:, kc, f2 * 128:(f2 + 1) * 128], xn[:, kc, cs], kc == 0, kc == KC - 1,
                           [f"ring{s}", f"xn{hf}_{kc}"], [PSN[bg]], kc == KC - 1)
                    TT("dve", sg[pi][:], PS[bg][:], rstd[hf][:], ALU.mult, [PSN[bg], f"rstd{hf}"], [f"sg{pi}"])
                    ACT(sg[pi][:], sg[pi][:], AF.Silu, [f"sg{pi}"], [f"sg{pi}"])
                    TT("dve", at[pi][:], PS[ba][:], rstd[hf][:], ALU.mult, [PSN[ba], f"rstd{hf}"], [f"at{pi}"])
                    TT("dve", h_all[:, f, cs], at[pi][:], sg[pi][:], ALU.mult, [f"at{pi}", f"sg{pi}"], [f"h{hf}"])

            if not skip_xw0:
                make_xw(0, ni)
            fence()
            make_xw(1, ni)
            for u in range(FC // 2):
                p1_load(u)
                for hf in range(2):
                    p1_groups(u, hf)
            w2s = {}

            def p2_load(m):
                s = next_slot()
                w2 = ring[s][:, 0:FC * 128].rearrange("p (f n) -> p f n", f=FC)
                wload(s, w2, wov[:, :, m * 128:(m + 1) * 128])
                w2s[m] = (s, w2)

            def p2_group(m, hf):
                s, w2 = w2s[m]
                cs = slice(hf * HT, (hf + 1) * HT)
                bo = 4 + po_i[0] % 2
                po_i[0] += 1
                for f in range(FC):
                    MM(PS[bo][:], w2[:, f, :], h_all[:, f, cs], f == 0, f == FC - 1, [f"ring{s}", f"h{hf}"], [PSN[bo]], f == FC - 1)
                flush_hooks()
                STT("dve", xT[:, m, cs], PS[bo][:], 0.5, xT[:, m, cs], ALU.mult, ALU.add, [PSN[bo], f"xT{hf}"], [f"xT{hf}"])
                stat_hook(hf, m, m == 0, m == KC - 1, defer=True)

            NL = len(tail_cbs) if tail_cbs else 3
            cbs = list(tail_cbs) if tail_cbs else [after_h0, None, None]
            for m in range(KC - NL):
                p2_load(m)
                for hf in range(2):
                    p2_group(m, hf)
            for m in range(KC - NL, KC):
                p2_load(m)
            for m in range(KC - NL, KC):
                p2_group(m, 0)
            for i, m in enumerate(range(KC - NL, KC)):
                p2_group(m, 1)
                if i == NL - 1:
                    flush_hooks()
                if cbs[i] is not None:
                    cbs[i]()
            flush_hooks()

        def mixer(l, hf, skip_norm=False, after_onorm=None):
            cs = slice(hf * HT, (hf + 1) * HT)
            XNS = [f"xn{hf}_{c}" for c in range(KC)]
            if not skip_norm:
                norm_to_xn(hf, 3 * l + 1)
            fence()
            wv = w_in[l:l + 1, :, :].rearrange("o (kc p) n -> p (o kc) n", p=128)
            wov = w_out[l:l + 1, :, :].rearrange("o (j p) m -> p (o j) m", p=128)
            SN = f"S{l}"

            def unit(c0, ncols):
                s = next_slot()
                v = ring[s][:, 0:KC * ncols].rearrange("p (kc n) -> p kc n", kc=KC)
                wload(s, v, wv[:, :, c0:c0 + ncols])
                return s, v

            def fm_group(s, w, c0, b):
                for kc in range(KC):
                    MM(PS[b][:], w[:, kc, c0:c0 + 128], xn[:, kc, cs], kc == 0, kc == KC - 1, [f"ring{s}", XNS[kc]], [PSN[b]], kc == KC - 1)

            def tm_group(s, w, c0, n, blk, b):
                for kc in range(KC):
                    MM(PS[b][:, 0:n], xn[:, kc, hf * HT + blk * 128:hf * HT + (blk + 1) * 128], w[:, kc, c0:c0 + n], kc == 0, kc == KC - 1,
                       [f"ring{s}", XNS[kc]], [PSN[b]], kc == KC - 1)

            s, wr = unit(2560, 16)
            for kc in range(KC):
                MM(PS[7][0:16, :], wr[:, kc, :], xn[:, kc, cs], kc == 0, kc == KC - 1, [f"ring{s}", XNS[kc]], [PSN[7]], kc == KC - 1)
            CP("act", rT[0:16, :], PS[7][0:16, :], [PSN[7]], ["rT"])
            sqk, wqk = unit(1024, 512)
            for c in range(2):
                fm_group(sqk, wqk, c * 128, c)
                ACT(qT[:, c, :], PS[c][:], AF.Copy, [PSN[c]], ["qT"], scale=0.125)
            for blk in range(4):
                b = 2 + blk // 2
                co = (blk % 2) * 256
                MM(PS[b][:, co:co + 256], rT[0:16, blk * 128:(blk + 1) * 128], wgk[l][:], True, True, ["rT", f"wgk{l}"], [PSN[b]], True)
            for blk in range(4):
                b = 2 + blk // 2
                co = (blk % 2) * 256
                i2 = blk % 2
                TT("dve", xgb[i2][:], PS[b][:, co:co + 256], bgk[l][:], ALU.add, [PSN[b], f"bgk{l}"], [f"xg{i2}"])
                ACT(xgb[i2][:], xgb[i2][:], AF.Exp, [f"xg{i2}"], [f"xg{i2}"], scale=-1.0)
                ACT(spa[:, blk, :], xgb[i2][:], AF.Ln, [f"xg{i2}"], [f"spa{blk}"], bias=1.0)
            s, wvb = unit(1536, 512)
            for blk in range(4):
                b = 4 + blk % 2
                tm_group(s, wvb, 0, 512, blk, b)
                CP("act" if blk % 2 else "dve", vtok[:, blk, :], PS[b][:], [PSN[b]], [f"vtok{blk}"])
            for blk in range(4):
                co = (blk // 2) * 256
                bE = 2 + blk % 2
                MM(PS[bE][:, co:co + 256], tri[:], spa[:, blk, :], True, True, ["tri", f"spa{blk}"], [PSN[bE]], True)
                for j in range(2):
                    MM(PS[1][:, blk * 4 + 2 * j:blk * 4 + 2 * j + 2], spa[:, blk, j * 128:(j + 1) * 128], ci[:], True, True,
                       ["ci", f"spa{blk}"], [PSN[1]], j == 1)
                ACT(wdec[:, blk, :], PS[bE][:, co:co + 256], AF.Exp, [PSN[bE]], ["wdec"], scale=-1.0 / 16)
            ACT(dec[:], PS[1][:, 0:16], AF.Exp, [PSN[1]], ["dec"], scale=-1.0 / 16)
            for blk in range(4):
                b = blk % 2
                tm_group(sqk, wqk, 256, 256, blk, b)
                TT("dve", kdec[:, blk, :], PS[b][:, 0:256], wdec[:, blk, :], ALU.mult, [PSN[b], "wdec"], [f"kdec{blk}"])

            s, wva = unit(512, 512)
            for blk in range(4):
                b = 4 + blk
                tm_group(s, wva, 0, 512, blk, b)
                ACT(junk[:], PS[b][:], AF.Square, [PSN[b]], ["junk", "vst"], accum_out=vst[:, blk:blk + 1])
                rsqrt_to(vst[:, 4 + blk:5 + blk], vst[:, blk:blk + 1], 1.0 / 512, ["vst"], ["vst"], ln_buf=vst[:, blk:blk + 1], ln_name="vst")
                STT("dve", vn[:, blk, :], PS[b][:], vst[:, 4 + blk:5 + blk], normv[l][:], ALU.mult, ALU.mult,
                    [PSN[b], "vst", f"normv{l}"], ["vn"])
            fill = []
            su, wu = unit(0, 512)

            def f_u(c):
                b = 6 + c % 2
                fm_group(su, wu, c * 128, b)
                CP("act", uT[:, c, :], PS[b][:], [PSN[b]], ["uT"])

            def f_z(j):
                b = 6 + j % 2
                for blk in range(4):
                    for hh in range(2):
                        g = 2 * j + hh
                        MM(PS[b][hh * 64:(hh + 1) * 64, blk * 128:(blk + 1) * 128], vn[:, blk, g * 64:(g + 1) * 64], wsT[l][:, g, :], True, True,
                           ["vn", f"wsT{l}"], [PSN[b]], (blk == 3 and hh == 1), tile_position=(0, hh * 64))
                i2 = j % 2
                TT("dve", tmp[i2][:].rearrange("p (b t) -> p b t", b=4), PS[b][:].rearrange("p (b t) -> p b t", b=4),
                   bsT[l][:, j:j + 1, :].to_broadcast([128, 4, 128]), ALU.add, [PSN[b], f"bsT{l}"], [f"tmp{i2}"])
                TT("dve", yT[:, j, :], tmp[i2][:], uT[:, j, :], ALU.mult, [f"tmp{i2}", "uT"], [f"yT{j}"])

            gunit = []

            def f_g(c):
                if not gunit:
                    gunit.append(unit(2048, 512))
                sg_, wgg = gunit[0]
                b = 6 + c % 2
                fm_group(sg_, wgg, c * 128, b)
                ACT(sgT[:, c, :], PS[b][:], AF.Silu, [PSN[b]], ["sgT"])

            fill = [lambda c=c: f_u(c) for c in range(4)] + [lambda j=j: f_z(j) for j in range(4)] + [lambda c=c: f_g(c) for c in range(4)]

            def upd(c):
                blk, ch = c // 2, c % 2
                rows = slice(ch * 64, (ch + 1) * 64)
                b = 4 + c % 2
                for j in range(2):
                    for hh in range(2):
                        hd = 2 * j + hh
                        MM(PS[b][hh * 64:(hh + 1) * 64, j * 128:(j + 1) * 128], kdec[rows, blk, hd * 64:(hd + 1) * 64],
                           vtok[rows, blk, hd * 128:(hd + 1) * 128], True, True,
                           [f"kdec{blk}", f"vtok{blk}"], [PSN[b]], (j == 1 and hh == 1), tile_position=(ch * 64, hh * 64))

            def state(c):
                blk, ch = c // 2, c % 2
                b = 4 + c % 2
                for j in range(2):
                    col = blk * 4 + 2 * j + ch
                    STT("dve", Sst[l][:, j, :], Sst[l][:, j, :], dec[:, col:col + 1],
                        PS[b][:, j * 128:(j + 1) * 128], ALU.mult, ALU.add, [SN, "dec", PSN[b]], [SN])
                CP("act", Sbf[c % 2][:], Sst[l][:], [SN], [f"Sbf{c % 2}"])

            def omm(c):
                for hd in range(4):
                    j, hh = hd // 2, hd % 2
                    pr = slice(hh * 64, (hh + 1) * 64)
                    MM(PS[hd][:, c * 64:(c + 1) * 64], Sbf[c % 2][pr, j, :], qT[pr, j, c * 64:(c + 1) * 64], True, True,
                       [f"Sbf{c % 2}", "qT"], [PSN[hd]], hd == 3)

            upd(0)
            upd(1)
            for c in range(8):
                state(c)
                nf = 2 if c < 3 else 1
                for _ in range(nf):
                    if len(fill) > 1:
                        fill.pop(0)()
                if c + 2 < 8:
                    upd(c + 2)
                omm(c)
            osq4 = [(osq[0], "osq0"), (osq[1], "osq1"), (junk, "junk"), (rT, "rT")]
            for hd in range(4):
                ACT(osq4[hd][0][:], PS[hd][:], AF.Square, [PSN[hd]], [osq4[hd][1]])
            while fill:
                fill.pop(0)()
            for hd in range(4):
                MM(PS[4 + hd][:], ones_bf[:], osq4[hd][0][:], True, True, ["ones", osq4[hd][1]], [PSN[4 + hd]], True)
            for hd in range(4):
                i2 = hd % 2
                rsqrt_to(ornd[i2][:], PS[4 + hd][:], 1.0 / 128, [PSN[4 + hd]], [f"sg{i2}"], ln_buf=lnb2[i2][:], ln_name=f"at{i2}")
                STT("dve", tmp[i2][:], PS[hd][:], normo[:, l * 4 + hd:l * 4 + hd + 1], ornd[i2][:], ALU.mult, ALU.mult,
                    [PSN[hd], "normo", f"sg{i2}"], [f"tmp{i2}"])
                TT("dve", yT[:, 4 + hd, :], tmp[i2][:], sgT[:, hd, :], ALU.mult, [f"tmp{i2}", "sgT"], [f"yT{4 + hd}"])
            if after_onorm is not None:
                after_onorm()
            for mu in range(2):
                s = next_slot()
                w2 = ring[s][:, 0:KC * 512].rearrange("p (j n) -> p j n", j=KC)
                wload(s, w2, wov[:, :, mu * 512:(mu + 1) * 512])
                for m4 in range(4):
                    m = mu * 4 + m4
                    b = 4 + m % 2
                    for j in range(KC):
                        MM(PS[b][:], w2[:, j, m4 * 128:(m4 + 1) * 128], yT[:, j, :], j == 0, j == KC - 1, [f"ring{s}", f"yT{j}"], [PSN[b]], j == KC - 1)
                    flush_hooks()
                    TT("dve", xT[:, m, cs], PS[b][:], xT[:, m, cs], ALU.add, [PSN[b], f"xT{hf}"], [f"xT{hf}"])
                    stat_hook(hf, m, m == 0, m == KC - 1, defer=True)
            flush_hooks()

        out_toks = []

        XNALL = [f"xn{h_}_{c}" for h_ in range(2) for c in range(KC)]
        yf0 = xn[:, :, :].rearrange("p c t -> p (c t)").bitcast(F32).rearrange("p (c t) -> p c t", c=KC)

        def final_stt(hf, ybuf, ynames):
            ni = 3 * L
            cs = slice(hf * HT, (hf + 1) * HT)
            for c in range(KC):
                STT("dve", ybuf[:, c, :], xT[:, c, cs], nw[:, ni * KC + c:ni * KC + c + 1], rstd[hf][:], ALU.mult, ALU.mult,
                    [f"xT{hf}", f"rstd{hf}", "nw"], ynames)

        def final_out(ti, hf, ybuf, ynames, banks):
            for blk in range(4):
                st = stg[blk % 2]
                sn = f"stg{blk % 2}"
                for g in range(2):
                    bank = banks[g]
                    for c4 in range(4):
                        c = g * 4 + c4
                        TR(PS[bank][:, c4 * 128:(c4 + 1) * 128], ybuf[:, c, blk * 128:(blk + 1) * 128], ynames, [PSN[bank]], c4 == 3)
                    CP("dve" if g == 0 else "act", st[:, g * 512:(g + 1) * 512], PS[bank][:], [PSN[bank]], [sn])
                r0 = ti * T + hf * HT + blk * 128
                out_toks.append(S.dma("sp", f"oq{blk % 2}", lambda h, st=st, r0=r0: h.dma_start(out=out[r0:r0 + 128, :], in_=st[:]), [sn], ()))

        def final_h1(ti):
            fence()
            final_stt(1, yf, ["yf"])
            final_out(ti, 1, yf, ["yf"], (6, 7))

        for ti in range(n_tiles):
            if ti == 0:
                for hf in range(2):
                    load_x_half(ti, hf)
                for hf in range(2):
                    rms_full(hf)
            else:
                load_x_half(ti, 1)
                rms_full(1)
            for l in range(n_layers):
                ffn(l, ff1_w_in, ff1_w_out, 3 * l, skip_xw0=(l > 0 or ti > 0),
                    after_h0=(None if (ti == 0 and l == 0) else (lambda l=l: norm_to_xn(0, 3 * l + 1))))
                if ti == 0 and l == 0:
                    consts_B()
                mixer(l, 0, skip_norm=not (ti == 0 and l == 0), after_onorm=lambda l=l: norm_to_xn(1, 3 * l + 1))
                mixer(l, 1, skip_norm=True, after_onorm=lambda l=l: make_xw(0, 3 * l + 2))
                if l + 1 < n_layers:
                    ffn(l, ff2_w_in, ff2_w_out, 3 * l + 2, skip_xw0=True, after_h0=(lambda l=l: make_xw(0, 3 * (l + 1))))
                else:
                    def cb_next(ti=ti):
                        if ti + 1 < n_tiles:
                            load_x_half(ti + 1, 0, banks=(0, 1))
                            rms_full(0)
                            make_xw(0, 0)
                    ffn(l, ff2_w_in, ff2_w_out, 3 * l + 2, skip_xw0=True,
                        tail_cbs=[lambda: final_stt(0, yf0, XNALL), None,
                                  lambda ti=ti: final_out(ti, 0, yf0, XNALL, (2, 3)), cb_next])
            final_h1(ti)
        last = {}
        for s, v in out_toks:
            last[s] = max(last.get(s, 0), v)
        S.wait_all("sp", list(last.items()))
        with nc.allow_non_contiguous_dma(reason="tiny constant loads"):
            S.run()
    return nc


_NC_CACHE = {}


def kernel(**inputs):
    if "nc" not in _NC_CACHE:
        _NC_CACHE["nc"] = build()
    nc = _NC_CACHE["nc"]
    x = np.ascontiguousarray(np.asarray(inputs["x"], dtype=np.float32))
    shared = {k: np.ascontiguousarray(np.asarray(v, dtype=np.float32)) for k, v in inputs.items() if k != "x"}
    in_maps = []
    for b in range(N_CORES):
        m = dict(shared)
        m["x"] = x[b]
        in_maps.append(m)
    res = run_bass_kernel_spmd(nc, in_maps, core_ids=list(range(N_CORES)))
    return np.stack([np.asarray(r["out"], dtype=np.float32) for r in res.results], axis=0)
```
